# Optimizing a Trainium2 kernel written in Bass

```python
import math
import jax
import jax.numpy as jnp
from jax import lax
import numpy as np

D_MODEL = 1024
BATCH = 8
SEQ = 2048
DEPTH = 2

HEAD_DIM = 64
ROPE_THETA = 10000.0
RMS_EPS = 1e-6
NEG_INF = -1e30
BLOCK = 128

A_Q_HEADS = 8
A_KV_HEADS = 2
A_GROUP = A_Q_HEADS // A_KV_HEADS
WINDOW = 128
A_Q_WIDTH = A_Q_HEADS * HEAD_DIM
A_KV_WIDTH = A_KV_HEADS * HEAD_DIM

HY_WIDTH = D_MODEL // 2
HY_ORDER = 2
HY_DIRS = 2
HY_SHORT = 3
HY_BANDS = 16
HY_POS_DIM = 1 + 2 * HY_BANDS
HY_FILTER_HIDDEN = 64
HY_DECAY_TARGET = 1e-2
HY_FAST_PCT = 0.3
HY_SLOW_PCT = 1.5

C_HEADS = 4
C_WIDTH = C_HEADS * 2 * HEAD_DIM

N_BRANCH = 3
IN_SIZES = (A_Q_WIDTH, A_KV_WIDTH, A_KV_WIDTH, (HY_ORDER + 1) * HY_WIDTH, C_WIDTH, C_WIDTH, C_WIDTH, N_BRANCH * D_MODEL)
IN_COLS = sum(IN_SIZES)
IN_SPLITS = tuple(sum(IN_SIZES[:i + 1]) for i in range(len(IN_SIZES) - 1))

D_FF = -(-8 * D_MODEL // (3 * 256)) * 256

kernel_name = 'hybrid_gated_window_hyena_diffattn_encoder'


def rms_norm(x, g):
    xf = x.astype(jnp.float32)
    y = xf * lax.rsqrt(jnp.mean(xf * xf, axis=-1, keepdims=True) + RMS_EPS)
    return (y * g.astype(jnp.float32)).astype(x.dtype)


def rope_tables(seq):
    pos = jnp.arange(seq, dtype=jnp.float32)
    inv = ROPE_THETA ** (-jnp.arange(0, HEAD_DIM, 2, dtype=jnp.float32) / HEAD_DIM)
    ang = pos[:, None] * inv[None, :]
    return jnp.cos(ang), jnp.sin(ang)


def apply_rope(x, cos, sin):
    c = cos[None, :, None, :].astype(x.dtype)
    s = sin[None, :, None, :].astype(x.dtype)
    x1, x2 = jnp.split(x, 2, axis=-1)
    return jnp.concatenate([x1 * c - x2 * s, x2 * c + x1 * s], axis=-1)


def window_attention(q, k, v, sink):
    B, S = q.shape[0], q.shape[1]
    nb = S // BLOCK
    span = BLOCK + 2 * WINDOW
    qb = q.reshape(B, nb, BLOCK, A_KV_HEADS, A_GROUP, HEAD_DIM)
    pad = ((0, 0), (WINDOW, WINDOW), (0, 0), (0, 0))
    kp = jnp.pad(k, pad)
    vp = jnp.pad(v, pad)
    idx = jnp.arange(nb)[:, None] * BLOCK + jnp.arange(span)[None, :]
    kb = kp[:, idx]
    vb = vp[:, idx]
    s = jnp.einsum('bnqhgd,bnkhd->bnhgqk', qb, kb).astype(jnp.float32) / math.sqrt(HEAD_DIM)
    q_pos = jnp.arange(nb)[:, None] * BLOCK + jnp.arange(BLOCK)[None, :]
    k_pos = idx - WINDOW
    dist = k_pos[:, None, :] - q_pos[:, :, None]
    valid = (jnp.abs(dist) <= WINDOW) & (k_pos[:, None, :] >= 0) & (k_pos[:, None, :] < S)
    s = jnp.where(valid[None, :, None, None], s, NEG_INF)
    sink_b = jnp.broadcast_to(sink.astype(jnp.float32).reshape(A_KV_HEADS, A_GROUP)[None, None, :, :, None, None], s.shape[:-1] + (1,))
    p = jax.nn.softmax(jnp.concatenate([s, sink_b], axis=-1), axis=-1)[..., :-1]
    o = jnp.einsum('bnhgqk,bnkhd->bnqhgd', p.astype(v.dtype), vb)
    return o.reshape(B, S, A_Q_HEADS * HEAD_DIM)


def diff_attention(q, k, v, lam):
    B, S = q.shape[0], q.shape[1]
    nb = S // BLOCK
    qb = q.reshape(B, nb, BLOCK, C_HEADS, 2, HEAD_DIM).transpose(1, 0, 3, 4, 2, 5)
    kt = k.transpose(0, 2, 3, 1, 4)
    vt = v.transpose(0, 2, 1, 3)

    def one_block(qblk):
        s = jnp.einsum('bhcqd,bhckd->bhcqk', qblk, kt).astype(jnp.float32) / math.sqrt(HEAD_DIM)
        p = jax.nn.softmax(s, axis=-1)
        a = p[:, :, 0] - lam * p[:, :, 1]
        return jnp.einsum('bhqk,bhkd->bhqd', a.astype(v.dtype), vt)

    o = lax.map(one_block, qb)
    return o.transpose(1, 0, 3, 2, 4).reshape(B, S, C_HEADS, 2 * HEAD_DIM)


def short_conv(u, w, b):
    L = u.shape[1]
    up = jnp.pad(u, ((0, 0), (1, 1), (0, 0)))
    return up[:, :L] * w[0] + up[:, 1:L + 1] * w[1] + up[:, 2:] * w[2] + b


def hyena_filters(L, w1, b1, f1, w2, b2, f2, w3):
    f32 = jnp.float32
    pos = jnp.arange(L, dtype=f32)
    t = pos / max(L - 1, 1)
    bands = jnp.arange(1, HY_BANDS + 1, dtype=f32)
    ang = 2.0 * math.pi * pos[:, None] * bands[None, :] / L
    z = jnp.concatenate([t[:, None], jnp.cos(ang), jnp.sin(ang)], axis=-1)
    h = jnp.sin(f1.astype(f32) * (z @ w1.astype(f32) + b1.astype(f32)))
    h = jnp.sin(f2.astype(f32) * (h @ w2.astype(f32) + b2.astype(f32)))
    h = (h @ w3.astype(f32)).reshape(L, HY_ORDER, HY_DIRS, HY_WIDTH)
    min_decay = math.log(HY_DECAY_TARGET) / HY_SLOW_PCT
    max_decay = math.log(HY_DECAY_TARGET) / HY_FAST_PCT
    deltas = jnp.abs(jnp.linspace(min_decay, max_decay, HY_WIDTH, dtype=f32))
    decay = jnp.exp(-t[:, None] * deltas[None, :])
    h = h * decay[:, None, None, :]
    fwd, bwd = h[:, :, 0], h[:, :, 1]
    c = jnp.concatenate([fwd[:1] + bwd[:1], fwd[1:], jnp.zeros_like(fwd[:1]), bwd[1:][::-1]], axis=0)
    return c * lax.rsqrt(jnp.sum(c * c, axis=0, keepdims=True) + RMS_EPS)


def fft_conv(u, cf):
    L = u.shape[1]
    U = jnp.fft.rfft(u.astype(jnp.float32), n=2 * L, axis=1)
    y = jnp.fft.irfft(U * cf[None], n=2 * L, axis=1)[:, :L]
    return y.astype(u.dtype)


def hyena_branch(u, conv_w, conv_b, w1, b1, f1, w2, b2, f2, w3, hy_d):
    L = u.shape[1]
    u = short_conv(u, conv_w, conv_b)
    v, x1, x2 = jnp.split(u, 3, axis=-1)
    cf = jnp.fft.rfft(hyena_filters(L, w1, b1, f1, w2, b2, f2, w3), axis=0)
    z = x1 * (fft_conv(v, cf[:, 0]) + hy_d[0] * v)
    z = x2 * (fft_conv(z, cf[:, 1]) + hy_d[1] * z)
    return z


def setup_inputs(seed: int = 0) -> dict:
    key = jax.random.key(seed)
    ks = jax.random.split(key, 40)

    def nrm(i, shape, scale):
        return jax.random.normal(ks[i], shape, dtype=jnp.float32) * scale

    def gain(i, shape):
        return 1.0 + nrm(i, shape, 0.05)

    L = DEPTH
    return {
        'x': nrm(0, (BATCH, SEQ, D_MODEL), 1.0),
        'norm1_g': gain(1, (L, D_MODEL)),
        'w_in': nrm(2, (L, D_MODEL, IN_COLS), D_MODEL ** -0.5),
        'sink_a': nrm(3, (L, A_Q_HEADS), 0.5),
        'qn_a': gain(4, (L, HEAD_DIM)),
        'kn_a': gain(5, (L, HEAD_DIM)),
        'conv_w': nrm(6, (L, HY_SHORT, (HY_ORDER + 1) * HY_WIDTH), 0.6),
        'conv_b': nrm(7, (L, (HY_ORDER + 1) * HY_WIDTH), 0.01),
        'filt_w1': nrm(8, (L, HY_POS_DIM, HY_FILTER_HIDDEN), HY_POS_DIM ** -0.5),
        'filt_b1': nrm(9, (L, HY_FILTER_HIDDEN), 0.1),
        'filt_f1': gain(10, (L, HY_FILTER_HIDDEN)),
        'filt_w2': nrm(11, (L, HY_FILTER_HIDDEN, HY_FILTER_HIDDEN), HY_FILTER_HIDDEN ** -0.5),
        'filt_b2': nrm(12, (L, HY_FILTER_HIDDEN), 0.1),
        'filt_f2': gain(13, (L, HY_FILTER_HIDDEN)),
        'filt_w3': nrm(14, (L, HY_FILTER_HIDDEN, HY_ORDER * HY_DIRS * HY_WIDTH), HY_FILTER_HIDDEN ** -0.5),
        'hy_d': nrm(15, (L, HY_ORDER, HY_WIDTH), 0.1),
        'qn_c': gain(16, (L, HEAD_DIM)),
        'kn_c': gain(17, (L, HEAD_DIM)),
        'lam_q1': nrm(18, (L, HEAD_DIM), 0.1),
        'lam_k1': nrm(19, (L, HEAD_DIM), 0.1),
        'lam_q2': nrm(20, (L, HEAD_DIM), 0.1),
        'lam_k2': nrm(21, (L, HEAD_DIM), 0.1),
        'subln_c': gain(22, (L, 2 * HEAD_DIM)),
        'w_oa': nrm(23, (L, A_Q_WIDTH, D_MODEL), A_Q_WIDTH ** -0.5),
        'w_ob': nrm(24, (L, HY_WIDTH, D_MODEL), HY_WIDTH ** -0.5),
        'w_oc': nrm(25, (L, C_WIDTH, D_MODEL), C_WIDTH ** -0.5),
        'w_out': nrm(26, (L, D_MODEL, D_MODEL), D_MODEL ** -0.5),
        'norm2_g': gain(27, (L, D_MODEL)),
        'ffn_w1': nrm(28, (L, D_MODEL, D_FF), D_MODEL ** -0.5),
        'ffn_w3': nrm(29, (L, D_MODEL, D_FF), D_MODEL ** -0.5),
        'ffn_w2': nrm(30, (L, D_FF, D_MODEL), D_FF ** -0.5),
    }


def reference(x, norm1_g, w_in, sink_a, qn_a, kn_a, conv_w, conv_b, filt_w1, filt_b1, filt_f1, filt_w2, filt_b2, filt_f2, filt_w3, hy_d, qn_c, kn_c, lam_q1, lam_k1, lam_q2, lam_k2, subln_c, w_oa, w_ob, w_oc, w_out, norm2_g, ffn_w1, ffn_w3, ffn_w2):
    B, S = x.shape[0], x.shape[1]
    cos, sin = rope_tables(S)
    for l in range(DEPTH):
        h = rms_norm(x, norm1_g[l])
        proj = h @ w_in[l]
        qa, ka, va, hy_in, qc, kc, vc, gate_logits = jnp.split(proj, IN_SPLITS, axis=-1)

        qa = apply_rope(rms_norm(qa.reshape(B, S, A_Q_HEADS, HEAD_DIM), qn_a[l]), cos, sin)
        ka = apply_rope(rms_norm(ka.reshape(B, S, A_KV_HEADS, HEAD_DIM), kn_a[l]), cos, sin)
        va = va.reshape(B, S, A_KV_HEADS, HEAD_DIM)
        out_a = window_attention(qa, ka, va, sink_a[l])

        out_b = hyena_branch(hy_in, conv_w[l], conv_b[l], filt_w1[l], filt_b1[l], filt_f1[l], filt_w2[l], filt_b2[l], filt_f2[l], filt_w3[l], hy_d[l])

        lambda_init = 0.8 - 0.6 * math.exp(-0.3 * l)
        lam = (jnp.exp(jnp.sum(lam_q1[l].astype(jnp.float32) * lam_k1[l].astype(jnp.float32)))
               - jnp.exp(jnp.sum(lam_q2[l].astype(jnp.float32) * lam_k2[l].astype(jnp.float32)))
               + lambda_init)
        qc = apply_rope(rms_norm(qc.reshape(B, S, 2 * C_HEADS, HEAD_DIM), qn_c[l]), cos, sin).reshape(B, S, C_HEADS, 2, HEAD_DIM)
        kc = apply_rope(rms_norm(kc.reshape(B, S, 2 * C_HEADS, HEAD_DIM), kn_c[l]), cos, sin).reshape(B, S, C_HEADS, 2, HEAD_DIM)
        vc = vc.reshape(B, S, C_HEADS, 2 * HEAD_DIM)
        out_c = diff_attention(qc, kc, vc, lam)
        out_c = (rms_norm(out_c, subln_c[l]) * (1.0 - lambda_init)).reshape(B, S, C_WIDTH)

        g = jax.nn.sigmoid(gate_logits.astype(jnp.float32)).reshape(B, S, N_BRANCH, D_MODEL).astype(x.dtype)
        merged = (g[:, :, 0] * (out_a @ w_oa[l])
                  + g[:, :, 1] * (out_b @ w_ob[l])
                  + g[:, :, 2] * (out_c @ w_oc[l]))
        x = x + merged @ w_out[l]

        h2 = rms_norm(x, norm2_g[l])
        x = x + (jax.nn.silu(h2 @ ffn_w1[l]) * (h2 @ ffn_w3[l])) @ ffn_w2[l]
    return x
```

```python
import math
import types
from contextlib import ExitStack

import numpy as np
import ml_dtypes
import concourse.bass as bass
import concourse.mybir as mybir
from concourse.bass_utils import run_bass_kernel_spmd

F32 = mybir.dt.float32
BF16 = mybir.dt.bfloat16
AF = mybir.ActivationFunctionType
ALU = mybir.AluOpType
AX = mybir.AxisListType

ENG_NAMES = ("pe", "act", "dve", "pool", "sp")
SEM_CHUNK = 12000

S = 2048
D = 1024
NT = 16
DFF = 2816
NFC = 22
INC = 6912
OFF_QA, OFF_KA, OFF_VA, OFF_HY, OFF_QC, OFF_KC, OFF_VC, OFF_G = 0, 512, 640, 768, 2304, 2816, 3328, 3840
EPS = 1e-6
NFFT = 4096


class T:
    __slots__ = ("name", "w", "re", "rd")

    def __init__(self, name=""):
        self.name = name
        self.w = None
        self.re = {}
        self.rd = []


class Op:
    __slots__ = ("eng", "fn", "waits", "inc", "idx", "dma", "inc_no")

    def __init__(self, eng, fn, idx, dma=None):
        self.eng = eng
        self.fn = fn
        self.waits = []
        self.inc = False
        self.idx = idx
        self.dma = dma
        self.inc_no = None


def _freeze(fn):
    if fn is None or fn.__closure__ is None:
        return fn
    cells = []
    for c in fn.__closure__:
        try:
            cells.append(types.CellType(c.cell_contents))
        except ValueError:
            cells.append(c)
    return types.FunctionType(fn.__code__, fn.__globals__, fn.__name__, fn.__defaults__, tuple(cells))


class K:
    def __init__(self, nc):
        self.nc = nc
        self.ops = {e: [] for e in ENG_NAMES}
        self.dma_cnt = {}
        self.dma_last = {}
        self.same_eng_gap = 4

    def _need(self, op, tok):
        if tok is None or tok is op:
            return
        if tok.dma is None and tok.eng == op.eng and op.dma is None:
            if op.eng == "pe":
                return
            if op.idx - tok.idx >= self.same_eng_gap:
                return
        op.waits.append(tok)
        if tok.dma is None:
            tok.inc = True

    def _track(self, op, reads, writes):
        for t in reads:
            self._need(op, t.w)
        for t in writes:
            for r in t.re.values():
                self._need(op, r)
            for r in t.rd:
                self._need(op, r)
            self._need(op, t.w)
        for t in reads:
            if op.dma is None:
                t.re[op.eng] = op
            else:
                t.rd.append(op)
        for t in writes:
            t.w = op
            t.re = {}
            t.rd = []

    def op(self, eng, fn, reads=(), writes=()):
        o = Op(eng, _freeze(fn), len(self.ops[eng]))
        self.ops[eng].append(o)
        self._track(o, reads, writes)
        return o

    def dma(self, eng, out, in_, sem, reads=(), writes=(), **kw):
        cnt = self.dma_cnt.get(sem, 0) + 16
        self.dma_cnt[sem] = cnt
        o = Op(eng, lambda e: e.dma_start(out=out, in_=in_, **kw), len(self.ops[eng]), dma=(sem, cnt))
        self.ops[eng].append(o)
        prev = self.dma_last.get(sem)
        if prev is not None:
            o.waits.append(prev)
        self.dma_last[sem] = o
        self._track(o, reads, writes)
        return o

    def barrier(self):
        lasts = []
        for e in ENG_NAMES:
            for o in reversed(self.ops[e]):
                if o.dma is None and o.fn is not None:
                    lasts.append(o)
                    break
        dmas = {}
        for e in ENG_NAMES:
            for o in self.ops[e]:
                if o.dma is not None:
                    dmas[o.dma[0]] = o
        toks = lasts + list(dmas.values())
        for e in ENG_NAMES:
            o = Op(e, None, len(self.ops[e]))
            self.ops[e].append(o)
            for tk in toks:
                if tk.dma is None:
                    if tk.eng != e:
                        o.waits.append(tk)
                        tk.inc = True
                else:
                    o.waits.append(tk)

    def emit(self, final_waits=()):
        nc = self.nc
        with ExitStack() as es:
            eng_sems = {}
            for e in ENG_NAMES:
                n_inc = 0
                for o in self.ops[e]:
                    if o.dma is None and o.inc:
                        n_inc += 1
                        o.inc_no = n_inc
                nch = (n_inc + SEM_CHUNK - 1) // SEM_CHUNK
                eng_sems[e] = [es.enter_context(nc.semaphore(f"s_{e}_{i}")) for i in range(max(nch, 1))]
            dma_sems = {s: es.enter_context(nc.semaphore(f"d_{s}")) for s in self.dma_cnt}
            block = es.enter_context(nc.Block())

            def tok_sem(tok):
                if tok.dma is not None:
                    return dma_sems[tok.dma[0]], tok.dma[1], ("d", tok.dma[0])
                ch = (tok.inc_no - 1) // SEM_CHUNK
                return eng_sems[tok.eng][ch], tok.inc_no - ch * SEM_CHUNK, ("e", tok.eng, ch)

            def run(e, engobj):
                waited = {}
                for o in self.ops[e]:
                    for tok in o.waits:
                        sem, val, key = tok_sem(tok)
                        if waited.get(key, 0) >= val:
                            continue
                        waited[key] = val
                        engobj.wait_ge(sem, val)
                    if o.fn is None:
                        continue
                    ins = o.fn(engobj)
                    if o.dma is not None:
                        ins.then_inc(dma_sems[o.dma[0]], 16)
                    elif o.inc:
                        ch = (o.inc_no - 1) // SEM_CHUNK
                        ins.then_inc(eng_sems[e][ch], 1)
                if e == "sp":
                    for tok in final_waits:
                        sem, val, key = tok_sem(tok)
                        engobj.wait_ge(sem, val)

            @block.tensor
            def _(eng):
                run("pe", eng)

            @block.scalar
            def _(eng):
                run("act", eng)

            @block.vector
            def _(eng):
                run("dve", eng)

            @block.gpsimd
            def _(eng):
                run("pool", eng)

            @block.sync
            def _(eng):
                run("sp", eng)


def make_consts():
    bf = ml_dtypes.bfloat16
    c = {}
    c["c_ident"] = np.eye(128, dtype=np.float32).astype(bf)
    rot = np.zeros((128, 128), np.float32)
    for h in range(2):
        for d in range(64):
            if d < 32:
                rot[h * 64 + d + 32, h * 64 + d] = -1.0
            else:
                rot[h * 64 + d - 32, h * 64 + d] = 1.0
    c["c_rot"] = rot.astype(bf)
    blk = np.zeros((128, 128), np.float32)
    blk[:64, :64] = 1.0 / 64
    blk[64:, 64:] = 1.0 / 64
    c["c_blk"] = blk.astype(bf)
    kk = np.arange(128)[:, None]
    qq = np.arange(128)[None, :]
    c["c_mprev"] = (kk >= qq).astype(np.float32).astype(bf)
    c["c_mnext"] = (kk <= qq).astype(np.float32).astype(bf)
    pos = np.arange(S, dtype=np.float32)
    inv = (10000.0 ** (-np.arange(0, 64, 2, dtype=np.float32) / 64)).astype(np.float32)
    ang = pos[None, :] * inv[:, None]
    j = (np.arange(128) % 64) % 32
    c["c_cos"] = np.cos(ang)[j].astype(bf)
    c["c_sin"] = np.sin(ang)[j].astype(bf)
    t = pos / (S - 1)
    bands = np.arange(1, 17, dtype=np.float32)
    a2 = (2.0 * math.pi * pos[:, None] * bands[None, :] / S).astype(np.float32)
    z = np.concatenate([t[:, None], np.cos(a2), np.sin(a2)], axis=-1).astype(np.float32)
    c["c_zT"] = np.ascontiguousarray(z.T)
    min_decay = math.log(1e-2) / 1.5
    max_decay = math.log(1e-2) / 0.3
    deltas = np.abs(np.linspace(min_decay, max_decay, 512, dtype=np.float32))
    decay = np.exp(-t[:, None] * deltas[None, :]).astype(np.float32)
    c["c_decay"] = np.ascontiguousarray(decay.reshape(16, 128, 512).transpose(1, 0, 2))
    idx = np.arange(17 * 128, dtype=np.int64)
    prod = (idx[:, None] * idx[None, :]) % NFFT
    angf = 2.0 * math.pi * prod.astype(np.float64) / NFFT
    C = np.cos(angf).reshape(17, 128, 17, 128).transpose(2, 1, 0, 3)
    Sn = np.sin(angf).reshape(17, 128, 17, 128).transpose(2, 1, 0, 3)
    c["c_tc"] = np.ascontiguousarray(C).astype(np.float32).astype(bf).reshape(17, 128, 17 * 128)
    c["c_ts"] = np.ascontiguousarray(Sn).astype(np.float32).astype(bf).reshape(17, 128, 17 * 128)
    w = np.full(17 * 128, 2.0 / NFFT, np.float32)
    w[0] = 1.0 / NFFT
    w[2048] = 1.0 / NFFT
    c["c_wf"] = np.ascontiguousarray(w.reshape(17, 128).T)
    ns0 = np.full((128, 128), 0.5, np.float32)
    ns0[0, :] = 1.0
    nd0 = np.full((128, 128), 0.5, np.float32)
    nd0[0, :] = 0.0
    nh = np.full((128, 128), 0.5, np.float32)
    c["c_nrm"] = np.stack([ns0, nd0, nh]).astype(bf)
    sh = np.zeros((4, 128, 128), np.float32)
    for t_ in range(1, 128):
        sh[0, t_ - 1, t_] = 1.0
        sh[1, t_, t_ - 1] = 1.0
    sh[2, 127, 0] = 1.0
    sh[3, 0, 127] = 1.0
    c["c_shift"] = sh.astype(bf)
    return c


CONST_SPECS = {
    "c_ident": ([128, 128], BF16), "c_rot": ([128, 128], BF16), "c_blk": ([128, 128], BF16),
    "c_mprev": ([128, 128], BF16), "c_mnext": ([128, 128], BF16),
    "c_cos": ([128, S], BF16), "c_sin": ([128, S], BF16),
    "c_zT": ([33, S], F32), "c_decay": ([128, 16, 512], F32),
    "c_tc": ([17, 128, 17 * 128], BF16), "c_ts": ([17, 128, 17 * 128], BF16),
    "c_wf": ([128, 17], F32), "c_nrm": ([3, 128, 128], BF16), "c_shift": ([4, 128, 128], BF16),
}

PARAM_SHAPES = {
    "norm1_g": (2, 1024), "w_in": (2, 1024, 6912), "sink_a": (2, 8), "qn_a": (2, 64), "kn_a": (2, 64),
    "conv_w": (2, 3, 1536), "conv_b": (2, 1536), "filt_w1": (2, 33, 64), "filt_b1": (2, 64), "filt_f1": (2, 64),
    "filt_w2": (2, 64, 64), "filt_b2": (2, 64), "filt_f2": (2, 64), "filt_w3": (2, 64, 2048), "hy_d": (2, 2, 512),
    "qn_c": (2, 64), "kn_c": (2, 64), "lam_q1": (2, 64), "lam_k1": (2, 64), "lam_q2": (2, 64), "lam_k2": (2, 64),
    "subln_c": (2, 128), "w_oa": (2, 512, 1024), "w_ob": (2, 512, 1024), "w_oc": (2, 512, 1024),
    "w_out": (2, 1024, 1024), "norm2_g": (2, 1024), "ffn_w1": (2, 1024, 2816), "ffn_w3": (2, 1024, 2816),
    "ffn_w2": (2, 2816, 1024),
}


def build(layers=(0, 1), dbg=False, stop=99):
    nc = bass.Bass("TRN2", target_bir_lowering=False)
    P = {}
    x_in = nc.dram_tensor("x", [S, D], F32, kind="ExternalInput").ap()
    for n, shp in PARAM_SHAPES.items():
        P[n] = nc.dram_tensor(n, list(shp), F32, kind="ExternalInput").ap()
    C = {}
    for n, (shp, dt_) in CONST_SPECS.items():
        C[n] = nc.dram_tensor(n, shp, dt_, kind="ExternalInput").ap()
    y_out = nc.dram_tensor("y", [S, D], F32, kind="ExternalOutput").ap()
    sk = "ExternalOutput" if dbg else "Internal"
    gatesT = nc.dram_tensor("s_gates", [24, 128, S], BF16, kind=sk).ap()
    utok = nc.dram_tensor("s_utok", [3, S, 512], BF16, kind=sk).ap()
    oT = [nc.dram_tensor(f"s_o{n}T", [4, 128, S], BF16, kind=sk).ap() for n in "abc"]
    cf = nc.dram_tensor("s_cf", [2, 17, 2, 128, 512], F32, kind=sk).ap()

    es = ExitStack()
    AW = 53120
    arena = es.enter_context(nc.sbuf_tensor("arena", [128, AW], F32))
    psb = [es.enter_context(nc.psum_tensor(f"ps{i}", [128, 512], F32)) for i in range(8)]
    k = K(nc)
    Tps = [T(f"ps{i}") for i in range(8)]

    def f32v(off, n, parts=128, p0=0):
        return arena[p0:p0 + parts, off:off + n]

    def bfv(off, nbf, parts=128, p0=0):
        return arena[p0:p0 + parts, off:off + nbf // 2].bitcast(BF16)

    o = 0
    XRES = f32v(o, 16384).rearrange("p (n d) -> p n d", n=NT); o += 16384
    Tx = [T(f"x{n}") for n in range(NT)]
    HT_OFF = o
    HT = bfv(o, 16384).rearrange("p (c t) -> p c t", c=8); o += 8192
    ThT = T("hT")
    ident = bfv(o, 128); o += 64
    rot = bfv(o, 128); o += 64
    blk = bfv(o, 128); o += 64
    mprev = bfv(o, 128); o += 64
    mnext = bfv(o, 128); o += 64
    cosT = bfv(o, S); o += 1024
    sinT = bfv(o, S); o += 1024
    wf = f32v(o, 17); o += 17
    epsc = f32v(o, 1); o += 1
    onesc = f32v(o, 1); o += 1
    o += 1
    nrm = bfv(o, 384).rearrange("p (a m) -> p a m", a=3); o += 192
    shm = bfv(o, 512).rearrange("p (a m) -> p a m", a=4); o += 256
    Tconst = T("const")
    SM = o; o += 3072
    R0 = o
    RW = AW - R0
    assert RW >= 21000, RW

    rot_state = {"banks": list(range(8)), "i": 0}

    def bank():
        b = rot_state["banks"][rot_state["i"] % len(rot_state["banks"])]
        rot_state["i"] += 1
        return b

    def set_banks(lst):
        rot_state["banks"] = list(lst)
        rot_state["i"] = 0

    for nm, ap in (("c_ident", ident), ("c_rot", rot), ("c_blk", blk), ("c_mprev", mprev), ("c_mnext", mnext),
                   ("c_cos", cosT), ("c_sin", sinT), ("c_wf", wf)):
        k.dma("sp", ap, C[nm], "const", writes=[Tconst])
    k.dma("sp", nrm, C["c_nrm"].rearrange("a p m -> p a m"), "const", writes=[Tconst])
    k.dma("sp", shm, C["c_shift"].rearrange("a p m -> p a m"), "const", writes=[Tconst])
    k.op("dve", lambda e: e.memset(epsc, EPS), writes=[Tconst])
    k.op("dve", lambda e: e.memset(onesc, 1.0), writes=[Tconst])
    xin_v = x_in.rearrange("(n p) d -> p n d", p=128)
    for n in range(NT):
        k.dma("sp", XRES[:, n, :], xin_v[:, n, :], f"x{n % 4}", writes=[Tx[n]])

    def layer(l):
        so = SM
        def sm(n):
            nonlocal so
            v = f32v(so, n); so += n
            return v
        Tsm = T("small")
        gq = sm(8)
        sink = sm(8); esink = sm(8)
        convw = sm(36).rearrange("p (k j) -> p k j", k=3)
        convb = sm(12)
        lamv = sm(256).rearrange("p (a d) -> p a d", a=4)
        lamt = sm(8)
        subln = sm(128)
        hyd = sm(1024).rearrange("p (o c) -> p o c", o=2)
        fw1 = sm(64); fw2 = sm(64)
        fvec = sm(8)
        assert so <= SM + 3072
        for i, nm in enumerate(("qn_a", "kn_a", "qn_c", "kn_c")):
            src = P[nm][l].rearrange("(d o) -> d o", o=1)
            k.dma("sp", gq[0:64, 4 + i:5 + i], src, "small", writes=[Tsm])
            k.dma("sp", gq[64:128, 4 + i:5 + i], src, "small", writes=[Tsm])
        k.dma("sp", sink, P["sink_a"][l].partition_broadcast(128), "small", writes=[Tsm])
        for i, nm in enumerate(("lam_q1", "lam_k1", "lam_q2", "lam_k2")):
            k.dma("sp", lamv[:, i, :], P[nm][l].partition_broadcast(128), "small", writes=[Tsm])
        k.dma("sp", subln, P["subln_c"][l].partition_broadcast(128), "small", writes=[Tsm])
        k.dma("sp", hyd.rearrange("p o c -> p (o c)"), P["hy_d"][l].rearrange("o c -> (o c)").partition_broadcast(128), "small", writes=[Tsm])
        k.dma("sp", fw1[0:33, :], P["filt_w1"][l], "small", writes=[Tsm])
        k.dma("sp", fw2[0:64, :], P["filt_w2"][l], "small", writes=[Tsm])
        for i, nm in enumerate(("filt_b1", "filt_f1", "filt_b2", "filt_f2")):
            k.dma("sp", fvec[0:64, i:i + 1], P[nm][l].rearrange("(d o) -> d o", o=1), "small", writes=[Tsm])
        k.op("dve", lambda e: e.tensor_scalar(out=gq[:, 0:1], in0=gq[:, 4:5], scalar1=0.125, scalar2=None, op0=ALU.mult), reads=[Tsm], writes=[Tsm])
        k.op("dve", lambda e: e.tensor_copy(out=gq[:, 1:2], in_=gq[:, 5:6]), reads=[Tsm], writes=[Tsm])
        k.op("dve", lambda e: e.tensor_scalar(out=gq[:, 2:3], in0=gq[:, 6:7], scalar1=0.125, scalar2=None, op0=ALU.mult), reads=[Tsm], writes=[Tsm])
        k.op("dve", lambda e: e.tensor_copy(out=gq[:, 3:4], in_=gq[:, 7:8]), reads=[Tsm], writes=[Tsm])
        k.op("act", lambda e: e.activation(out=esink, in_=sink, func=AF.Exp), reads=[Tsm], writes=[Tsm])
        k.op("dve", lambda e: e.tensor_tensor(out=lamv[:, 0, :], in0=lamv[:, 0, :], in1=lamv[:, 1, :], op=ALU.mult), reads=[Tsm], writes=[Tsm])
        k.op("dve", lambda e: e.tensor_tensor(out=lamv[:, 2, :], in0=lamv[:, 2, :], in1=lamv[:, 3, :], op=ALU.mult), reads=[Tsm], writes=[Tsm])
        k.op("dve", lambda e: e.tensor_reduce(out=lamt[:, 0:1], in_=lamv[:, 0, :], axis=AX.X, op=ALU.add), reads=[Tsm], writes=[Tsm])
        k.op("dve", lambda e: e.tensor_reduce(out=lamt[:, 1:2], in_=lamv[:, 2, :], axis=AX.X, op=ALU.add), reads=[Tsm], writes=[Tsm])
        k.op("act", lambda e: e.activation(out=lamt[:, 2:4], in_=lamt[:, 0:2], func=AF.Exp), reads=[Tsm], writes=[Tsm])
        lam_init = 0.8 - 0.6 * math.exp(-0.3 * l)
        k.op("dve", lambda e: e.tensor_tensor(out=lamt[:, 4:5], in0=lamt[:, 3:4], in1=lamt[:, 2:3], op=ALU.subtract), reads=[Tsm], writes=[Tsm])
        k.op("dve", lambda e: e.tensor_scalar(out=lamt[:, 4:5], in0=lamt[:, 4:5], scalar1=-lam_init, scalar2=None, op0=ALU.add), reads=[Tsm], writes=[Tsm])
        k.op("dve", lambda e: e.tensor_scalar(out=fvec[0:64, 4:5], in0=fvec[0:64, 1:2], scalar1=1.0 / 3, scalar2=None, op0=ALU.mult), reads=[Tsm], writes=[Tsm])
        k.op("dve", lambda e: e.tensor_tensor(out=fvec[0:64, 5:6], in0=fvec[0:64, 4:5], in1=fvec[0:64, 0:1], op=ALU.mult), reads=[Tsm], writes=[Tsm])
        k.op("dve", lambda e: e.tensor_scalar(out=fvec[0:64, 6:7], in0=fvec[0:64, 3:4], scalar1=1.0 / 3, scalar2=None, op0=ALU.mult), reads=[Tsm], writes=[Tsm])
        k.op("dve", lambda e: e.tensor_tensor(out=fvec[0:64, 7:8], in0=fvec[0:64, 6:7], in1=fvec[0:64, 2:3], op=ALU.mult), reads=[Tsm], writes=[Tsm])

        def rms_to_hT(gname):
            ro = R0
            gb = f32v(ro, 1024); ro += 1024
            sq = f32v(ro, 1024); ro += 1024
            ss = f32v(ro, 16); ro += 16
            rs = f32v(ro, 16); ro += 16
            hb = [bfv(ro + 512 * i, 1024) for i in range(2)]; ro += 1024
            Tg, Tsq, Tss, Trs = T(), T(), T(), T()
            Thb = [T(), T()]
            k.dma("sp", gb, P[gname][l].partition_broadcast(128), "rg", writes=[Tg])
            for n in range(NT):
                k.op("act", lambda e, n=n: e.activation(out=sq, in_=XRES[:, n, :], func=AF.Square, accum_out=ss[:, n:n + 1]),
                     reads=[Tx[n]], writes=[Tsq, Tss])
            k.op("act", lambda e: e.activation(out=rs, in_=ss, func=AF.Sqrt, scale=1.0 / D, bias=epsc), reads=[Tss, Tconst], writes=[Trs])
            k.op("dve", lambda e: e.reciprocal(out=rs, in_=rs), reads=[Trs], writes=[Trs])
            for n in range(NT):
                h = hb[n % 2]
                k.op("dve", lambda e, n=n, h=h: e.scalar_tensor_tensor(out=h, in0=XRES[:, n, :], scalar=rs[:, n:n + 1], in1=gb, op0=ALU.mult, op1=ALU.mult),
                     reads=[Tx[n], Trs, Tg], writes=[Thb[n % 2]])
                b = bank()
                pT = psb[b][:].bitcast(BF16)
                for c in range(8):
                    k.op("pe", lambda e, c=c, h=h, pT=pT: e.transpose(out=pT[:, c * 128:(c + 1) * 128], in_=h[:, c * 128:(c + 1) * 128], identity=ident),
                         reads=[Thb[n % 2], Tconst], writes=[Tps[b]])
                k.op("act", lambda e, n=n, pT=pT: e.activation(out=HT[:, :, n * 128:(n + 1) * 128], in_=pT.rearrange("p (c t) -> p c t", c=8), func=AF.Copy),
                     reads=[Tps[b]], writes=[ThT])

        WB_OFF = R0
        wbuf = [bfv(WB_OFF + 2048 * i, 4096).rearrange("p (c n) -> p c n", c=8) for i in range(2)]
        Twb = [T("wb0"), T("wb1")]
        wstate = {"i": 0}

        def load_w(src_list):
            s = wstate["i"] % 2
            wstate["i"] += 1
            for (c0, ncol, src) in src_list:
                k.dma("pool", wbuf[s][:, :, c0:c0 + ncol], src.rearrange("(c p) n -> p c n", p=128), f"wb{s}", writes=[Twb[s]])
            return s

        win = P["w_in"][l]

        def proj_fm(s, c0, quad, b):
            for kc in range(8):
                k.op("pe", lambda e, kc=kc: e.matmul(psb[b][:], lhsT=wbuf[s][:, kc, c0:c0 + 128], rhs=HT[:, kc, quad * 512:(quad + 1) * 512], start=(kc == 0), stop=(kc == 7)),
                     reads=[Twb[s], ThT], writes=[Tps[b]])

        rms_to_hT("norm1_g")
        k.barrier()

        if stop <= 0:
            return
        ro = R0 + 4096
        gst = [bfv(ro + 1024 * i, 2048) for i in range(2)]; ro += 2048
        Tgst = [T(), T()]
        Tgates = [T() for _ in range(24)]
        for g4 in range(6):
            s = load_w([(0, 512, win[:, OFF_G + 512 * g4: OFF_G + 512 * (g4 + 1)])])
            for jj in range(4):
                j = g4 * 4 + jj
                st = gst[j % 2]
                for q in range(4):
                    b = bank()
                    proj_fm(s, jj * 128, q, b)
                    k.op("act", lambda e, b=b, st=st, q=q: e.activation(out=st[:, q * 512:(q + 1) * 512], in_=psb[b][:], func=AF.Sigmoid),
                         reads=[Tps[b]], writes=[Tgst[j % 2]])
                k.dma("sp", gatesT[j], st, f"gst{j % 2}", reads=[Tgst[j % 2]], writes=[Tgates[j]])

        if stop <= 1:
            return
        xb = bfv(ro, 8192).rearrange("p (n c) -> p n c", n=16); ro += 4096
        cwb = [f32v(ro + 512 * i, 512) for i in range(4)]; ro += 2048
        c1 = f32v(ro, 512); ro += 512
        c2 = f32v(ro, 512); ro += 512
        c3 = f32v(ro, 512); ro += 512
        ust = [bfv(ro + 256 * i, 512) for i in range(2)]; ro += 512
        assert ro <= AW, ro
        Txb = [T() for _ in range(16)]
        Tcwb, Tc1, Tc2, Tc3 = T(), T(), T(), T()
        Tust = [T(), T()]
        Tutok = [T() for _ in range(12)]
        for cc in range(3):
            s = load_w([(0, 512, win[:, OFF_HY + 512 * cc: OFF_HY + 512 * (cc + 1)])])
            for kk in range(3):
                k.dma("sp", cwb[kk], P["conv_w"][l][kk, cc * 512:(cc + 1) * 512].partition_broadcast(128), "cwb", writes=[Tcwb])
            k.dma("sp", cwb[3], P["conv_b"][l][cc * 512:(cc + 1) * 512].partition_broadcast(128), "cwb", writes=[Tcwb])
            for n in range(NT):
                b = bank()
                for kc in range(8):
                    k.op("pe", lambda e, kc=kc, n=n, b=b, s=s: e.matmul(psb[b][:], lhsT=HT[:, kc, n * 128:(n + 1) * 128], rhs=wbuf[s][:, kc, 0:512], start=(kc == 0), stop=(kc == 7)),
                         reads=[ThT, Twb[s]], writes=[Tps[b]])
                k.op("act", lambda e, n=n, b=b: e.activation(out=xb[:, n, :], in_=psb[b][:], func=AF.Copy), reads=[Tps[b]], writes=[Txb[n]])
            for n in range(NT):
                ba = bank()
                k.op("pe", lambda e, n=n, ba=ba: e.matmul(psb[ba][:], lhsT=shm[:, 0, :], rhs=xb[:, n, :], start=True, stop=(n == 0)), reads=[Tconst, Txb[n]], writes=[Tps[ba]])
                if n > 0:
                    k.op("pe", lambda e, n=n, ba=ba: e.matmul(psb[ba][:], lhsT=shm[:, 2, :], rhs=xb[:, n - 1, :], start=False, stop=True), reads=[Tconst, Txb[n - 1]], writes=[Tps[ba]])
                bb2 = bank()
                k.op("pe", lambda e, n=n, bb2=bb2: e.matmul(psb[bb2][:], lhsT=shm[:, 1, :], rhs=xb[:, n, :], start=True, stop=(n == NT - 1)), reads=[Tconst, Txb[n]], writes=[Tps[bb2]])
                if n < NT - 1:
                    k.op("pe", lambda e, n=n, bb2=bb2: e.matmul(psb[bb2][:], lhsT=shm[:, 3, :], rhs=xb[:, n + 1, :], start=False, stop=True), reads=[Tconst, Txb[n + 1]], writes=[Tps[bb2]])
                k.op("dve", lambda e, ba=ba: e.tensor_tensor(out=c1, in0=psb[ba][:], in1=cwb[0], op=ALU.mult), reads=[Tps[ba], Tcwb], writes=[Tc1])
                k.op("dve", lambda e, bb2=bb2: e.tensor_tensor(out=c2, in0=psb[bb2][:], in1=cwb[2], op=ALU.mult), reads=[Tps[bb2], Tcwb], writes=[Tc2])
                k.op("dve", lambda e, n=n: e.tensor_tensor(out=c3, in0=xb[:, n, :], in1=cwb[1], op=ALU.mult), reads=[Txb[n], Tcwb], writes=[Tc3])
                k.op("dve", lambda e: e.tensor_tensor(out=c3, in0=c3, in1=cwb[3], op=ALU.add), reads=[Tc3, Tcwb], writes=[Tc3])
                k.op("dve", lambda e: e.tensor_tensor(out=c1, in0=c1, in1=c2, op=ALU.add), reads=[Tc1, Tc2], writes=[Tc1])
                ui = n % 2
                k.op("dve", lambda e, ui=ui: e.tensor_tensor(out=ust[ui], in0=c1, in1=c3, op=ALU.add), reads=[Tc1, Tc3], writes=[Tust[ui]])
                k.dma("sp", utok[cc][n * 128:(n + 1) * 128, :], ust[ui], f"ust{ui}", reads=[Tust[ui]], writes=[Tutok[cc * 4 + (n % 4)]])
        k.barrier()

        if stop <= 2:
            return
        ro = R0 + 4096
        QP = {}
        QP["gqb"] = [bfv(ro + 256 * i, 512) for i in range(2)]; ro += 512
        QP["sqb"] = [bfv(ro + 256 * i, 512) for i in range(2)]; ro += 512
        QP["rs"] = [f32v(ro + 512 * i, 512) for i in range(2)]; ro += 1024
        QP["t1"] = [f32v(ro + 512 * i, 512) for i in range(2)]; ro += 1024
        QP["t2"] = [f32v(ro + 512 * i, 512) for i in range(2)]; ro += 1024
        QP["T"] = {nm: [T(), T()] for nm in ("gqb", "sqb", "rs", "t1", "t2")}
        QP["i"] = 0
        RO_A = ro

        def qk_post(b, dst, Tdst, gcol, quad):
            i = QP["i"] % 2
            QP["i"] += 1
            gqb, sqb, rs, t1, t2 = (QP[n][i] for n in ("gqb", "sqb", "rs", "t1", "t2"))
            Tg_, Ts_, Tr_, T1_, T2_ = (QP["T"][n][i] for n in ("gqb", "sqb", "rs", "t1", "t2"))
            k.op("act", lambda e: e.activation(out=gqb, in_=psb[b][:], func=AF.Copy, scale=gq[:, gcol:gcol + 1]), reads=[Tps[b], Tsm], writes=[Tg_])
            k.op("act", lambda e: e.activation(out=sqb, in_=psb[b][:], func=AF.Square), reads=[Tps[b]], writes=[Ts_])
            b2 = bank()
            k.op("pe", lambda e: e.matmul(psb[b2][:], lhsT=blk, rhs=sqb, start=True, stop=True), reads=[Ts_, Tconst], writes=[Tps[b2]])
            b3 = bank()
            k.op("pe", lambda e: e.matmul(psb[b3][:], lhsT=rot, rhs=gqb, start=True, stop=True), reads=[Tg_, Tconst], writes=[Tps[b3]])
            k.op("act", lambda e: e.activation(out=rs, in_=psb[b2][:], func=AF.Sqrt, bias=epsc), reads=[Tps[b2], Tconst], writes=[Tr_])
            k.op("dve", lambda e: e.reciprocal(out=rs, in_=rs), reads=[Tr_], writes=[Tr_])
            k.op("dve", lambda e: e.tensor_tensor(out=t1, in0=gqb, in1=cosT[:, quad * 512:(quad + 1) * 512], op=ALU.mult), reads=[Tg_, Tconst], writes=[T1_])
            k.op("dve", lambda e: e.tensor_tensor(out=t2, in0=psb[b3][:], in1=sinT[:, quad * 512:(quad + 1) * 512], op=ALU.mult), reads=[Tps[b3], Tconst], writes=[T2_])
            k.op("dve", lambda e: e.tensor_tensor(out=t1, in0=t1, in1=t2, op=ALU.add), reads=[T1_, T2_], writes=[T1_])
            k.op("dve", lambda e: e.tensor_tensor(out=dst, in0=t1, in1=rs, op=ALU.mult), reads=[T1_, Tr_], writes=[Tdst])

        ro = RO_A
        QA = bfv(ro, 8192).rearrange("p (c t) -> p c t", c=4); ro += 4096
        KA = bfv(ro, 4096).rearrange("p (c t) -> p c t", c=2); ro += 2048
        KZ = bfv(ro, 8192).rearrange("p (c z t) -> p c z t", c=2, z=2); ro += 4096
        TKZ = T()
        VA = bfv(ro, 2560).rearrange("p (n h d) -> p n h d", n=16, h=2); ro += 1280
        ET = [bfv(ro + 256 * i, 512) for i in range(4)]; ro += 1024
        zs = f32v(ro, 8); ro += 8
        oat = [bfv(ro + 256 * i, 512) for i in range(2)]; ro += 512
        oaTs = [bfv(ro, 2048).rearrange("p (c t) -> p c t", c=4) for i in range(2)]; ro += 1024
        assert ro <= AW, ro
        TQA, TKA, TVA = T(), T(), T()
        TET = [T() for _ in range(4)]
        Tzs = T()
        Toat = [T(), T()]
        ToaTs = [T(), T()]
        ToT = {n: [T() for _ in range(4)] for n in "abc"}
        for i_ in range(3):
            k.op("dve", lambda e, i_=i_: e.memset(VA.rearrange("p n h d -> p (n h d)")[:, i_ * 1024:min(2560, (i_ + 1) * 1024)], 1.0), writes=[TVA])
        s = load_w([(0, 512, win[:, OFF_QA:OFF_QA + 512])])
        for j in range(4):
            for q in range(4):
                b = bank()
                proj_fm(s, j * 128, q, b)
                qk_post(b, QA[:, j, q * 512:(q + 1) * 512], TQA, 0, q)
        s = load_w([(0, 64, win[:, OFF_KA:OFF_KA + 64]), (64, 64, win[:, OFF_KA:OFF_KA + 64]),
                    (128, 64, win[:, OFF_KA + 64:OFF_KA + 128]), (192, 64, win[:, OFF_KA + 64:OFF_KA + 128]),
                    (256, 128, win[:, OFF_VA:OFF_VA + 128])])
        for j in range(2):
            for q in range(4):
                b = bank()
                proj_fm(s, j * 128, q, b)
                qk_post(b, KA[:, j, q * 512:(q + 1) * 512], TKA, 1, q)
        for hk_ in range(2):
            for z_ in range(2):
                for q_ in range(2):
                    sl_ = slice(q_ * 1024, (q_ + 1) * 1024)
                    k.op("dve", lambda e, hk_=hk_, z_=z_, sl_=sl_: e.memset(KZ[:, hk_, z_, sl_], 0.0), writes=[TKZ])
        for hk_ in range(2):
            for z_ in range(2):
                for q_ in range(2):
                    sl_ = slice(q_ * 1024, (q_ + 1) * 1024)
                    k.op("dve", lambda e, hk_=hk_, z_=z_, sl_=sl_: e.tensor_copy(out=KZ[z_ * 64:(z_ + 1) * 64, hk_, z_, sl_], in_=KA[z_ * 64:(z_ + 1) * 64, hk_, sl_]), reads=[TKA], writes=[TKZ])
        for n in range(NT):
            b = bank()
            for kc in range(8):
                k.op("pe", lambda e, kc=kc, n=n, b=b: e.matmul(psb[b][:, 0:128], lhsT=HT[:, kc, n * 128:(n + 1) * 128], rhs=wbuf[s][:, kc, 256:384], start=(kc == 0), stop=(kc == 7)),
                     reads=[ThT, Twb[s]], writes=[Tps[b]])
            k.op("act", lambda e, n=n, b=b: e.activation(out=VA[:, n, :, 0:64], in_=psb[b][:, 0:128].rearrange("p (h d) -> p h d", h=2), func=AF.Copy),
                 reads=[Tps[b]], writes=[TVA])
        if stop == 25:
            return
        eti = 0
        for n in range(NT):
            chunks = [c for c in (n - 1, n, n + 1) if 0 <= c < NT]
            ot = oat[n % 2]
            for hk in range(2):
                ets = []
                for c in chunks:
                    b = bank()
                    for g in range(4):
                        H = 4 * hk + g
                        r0 = (H % 2) * 64
                        k.op("pe", lambda e, b=b, g=g, H=H, r0=r0, c=c, n=n, hk=hk: e.matmul(
                            psb[b][:, g * 128:(g + 1) * 128], lhsT=KZ[:, hk, H % 2, c * 128:(c + 1) * 128],
                            rhs=QA[:, H // 2, n * 128:(n + 1) * 128], start=True, stop=True),
                            reads=[TKZ, TQA], writes=[Tps[b]])
                    ei = eti % 4
                    eti += 1
                    k.op("act", lambda e, b=b, ei=ei: e.activation(out=ET[ei], in_=psb[b][:], func=AF.Exp), reads=[Tps[b]], writes=[TET[ei]])
                    if c != n:
                        mk = mprev if c < n else mnext
                        k.op("dve", lambda e, ei=ei, mk=mk: e.tensor_tensor(out=ET[ei].rearrange("p (g q) -> p g q", g=4), in0=ET[ei].rearrange("p (g q) -> p g q", g=4),
                                                                           in1=mk.unsqueeze(1).to_broadcast([128, 4, 128]), op=ALU.mult),
                             reads=[TET[ei], Tconst], writes=[TET[ei]])
                    ets.append(ei)
                b = bank()
                for g in range(4):
                    for ci, c in enumerate(chunks):
                        ei = ets[ci]
                        k.op("pe", lambda e, b=b, g=g, ei=ei, c=c, hk=hk, ci=ci, nchk=len(chunks): e.matmul(
                            psb[b][:, g * 80:g * 80 + 65], lhsT=ET[ei][:, g * 128:(g + 1) * 128], rhs=VA[:, c, hk, 0:65],
                            start=(ci == 0), stop=(ci == nchk - 1)),
                            reads=[TET[ei], TVA], writes=[Tps[b]])
                pv = psb[b][:, 0:320].rearrange("p (g d) -> p g d", g=4)
                k.op("dve", lambda e, pv=pv, hk=hk: e.tensor_tensor(out=zs[:, hk * 4:(hk + 1) * 4], in0=pv[:, :, 64], in1=esink[:, hk * 4:(hk + 1) * 4], op=ALU.add),
                     reads=[Tps[b], Tsm], writes=[Tzs])
                k.op("dve", lambda e, hk=hk: e.reciprocal(out=zs[:, hk * 4:(hk + 1) * 4], in_=zs[:, hk * 4:(hk + 1) * 4]), reads=[Tzs], writes=[Tzs])
                k.op("dve", lambda e, pv=pv, hk=hk, ot=ot: e.tensor_tensor(out=ot[:, hk * 256:(hk + 1) * 256].rearrange("p (g d) -> p g d", g=4), in0=pv[:, :, 0:64],
                                                                     in1=zs[:, hk * 4:(hk + 1) * 4].unsqueeze(2).to_broadcast([128, 4, 64]), op=ALU.mult),
                     reads=[Tps[b], Tzs], writes=[Toat[n % 2]])
            b = bank()
            pT = psb[b][:].bitcast(BF16)
            for c4 in range(4):
                k.op("pe", lambda e, c4=c4, pT=pT, ot=ot: e.transpose(out=pT[:, c4 * 128:(c4 + 1) * 128], in_=ot[:, c4 * 128:(c4 + 1) * 128], identity=ident),
                     reads=[Toat[n % 2], Tconst], writes=[Tps[b]])
            si = 0
            k.op("act", lambda e, pT=pT, si=si, n=n: e.activation(out=oaTs[si][:, :, (n % 4) * 128:(n % 4 + 1) * 128], in_=pT[:, 0:512].rearrange("p (c t) -> p c t", c=4), func=AF.Copy),
                 reads=[Tps[b]], writes=[ToaTs[si]])
            if n % 4 == 3:
                q = n // 4
                k.dma("sp", oT[0].rearrange("c p t -> p c t")[:, :, q * 512:(q + 1) * 512], oaTs[si], f"oaTs{si}", reads=[ToaTs[si]], writes=[ToT["a"][q]])
        k.barrier()

        if stop <= 3:
            return
        ro = RO_A
        QC = bfv(ro, 2048); ro += 1024
        KC = bfv(ro, 2048); ro += 1024
        KCZ = bfv(ro, 4096).rearrange("p (z t) -> p z t", z=2); ro += 2048
        TKCZ = T()
        VC = bfv(ro, 16 * 144).rearrange("p (n d) -> p n d", n=16); ro += 1152
        EC = [bfv(ro + 256 * i, 512) for i in range(3)]; ro += 768
        o01 = [f32v(ro + 128 * i, 128) for i in range(2)]; ro += 256
        zc = f32v(ro, 8); ro += 8
        och = [bfv(ro + 256 * i, 512) for i in range(2)]; ro += 512
        ocTs = [bfv(ro + 256 * i, 512) for i in range(2)]; ro += 512
        junk = f32v(ro, 128); ro += 128
        assert ro <= AW, ro
        TQC, TKC, TVC = T(), T(), T()
        TEC = [T() for _ in range(3)]
        To01 = [T(), T()]
        Tzc = T()
        Toch = [T(), T()]
        TocTs = [T(), T()]
        Tjunk = T()
        set_banks([0, 1, 2, 3])
        for i_ in range(3):
            k.op("dve", lambda e, i_=i_: e.memset(VC.rearrange("p n d -> p (n d)")[:, i_ * 1024:min(2304, (i_ + 1) * 1024)], 1.0), writes=[TVC])
        eci = 0
        for h in range(4):
            s = load_w([(0, 128, win[:, OFF_QC + 128 * h:OFF_QC + 128 * (h + 1)]),
                        (128, 128, win[:, OFF_KC + 128 * h:OFF_KC + 128 * (h + 1)]),
                        (256, 128, win[:, OFF_VC + 128 * h:OFF_VC + 128 * (h + 1)])])
            for q in range(4):
                b = bank()
                proj_fm(s, 0, q, b)
                qk_post(b, QC[:, q * 512:(q + 1) * 512], TQC, 2, q)
            for q in range(4):
                b = bank()
                proj_fm(s, 128, q, b)
                qk_post(b, KC[:, q * 512:(q + 1) * 512], TKC, 3, q)
            for z_ in range(2):
                for q_ in range(2):
                    sl_ = slice(q_ * 1024, (q_ + 1) * 1024)
                    k.op("dve", lambda e, z_=z_, sl_=sl_: e.memset(KCZ[:, z_, sl_], 0.0), writes=[TKCZ])
            for z_ in range(2):
                for q_ in range(2):
                    sl_ = slice(q_ * 1024, (q_ + 1) * 1024)
                    k.op("dve", lambda e, z_=z_, sl_=sl_: e.tensor_copy(out=KCZ[z_ * 64:(z_ + 1) * 64, z_, sl_], in_=KC[z_ * 64:(z_ + 1) * 64, sl_]), reads=[TKC], writes=[TKCZ])
            for n in range(NT):
                b = bank()
                for kc in range(8):
                    k.op("pe", lambda e, kc=kc, n=n, b=b: e.matmul(psb[b][:, 0:128], lhsT=HT[:, kc, n * 128:(n + 1) * 128], rhs=wbuf[s][:, kc, 256:384], start=(kc == 0), stop=(kc == 7)),
                         reads=[ThT, Twb[s]], writes=[Tps[b]])
                k.op("act", lambda e, n=n, b=b: e.activation(out=VC[:, n, 0:128], in_=psb[b][:, 0:128], func=AF.Copy), reads=[Tps[b]], writes=[TVC])
            for qd in range(4):
                oc = och[qd % 2]
                for m in range(2):
                    r0 = m * 64
                    for kc in range(NT):
                        b = bank()
                        k.op("pe", lambda e, b=b, kc=kc, qd=qd, m=m: e.matmul(psb[b][:], lhsT=KCZ[:, m, kc * 128:(kc + 1) * 128], rhs=QC[:, qd * 512:(qd + 1) * 512], start=True, stop=True),
                             reads=[TKCZ, TQC], writes=[Tps[b]])
                        ei = eci % 3
                        eci += 1
                        k.op("act", lambda e, b=b, ei=ei: e.activation(out=EC[ei], in_=psb[b][:], func=AF.Exp), reads=[Tps[b]], writes=[TEC[ei]])
                        for qt in range(4):
                            k.op("pe", lambda e, qt=qt, ei=ei, kc=kc: e.matmul(psb[4 + qt][:, 0:129], lhsT=EC[ei][:, qt * 128:(qt + 1) * 128], rhs=VC[:, kc, 0:129], start=(kc == 0), stop=(kc == NT - 1)),
                                 reads=[TEC[ei], TVC], writes=[Tps[4 + qt]])
                    for qt in range(4):
                        acc = psb[4 + qt]
                        k.op("dve", lambda e, acc=acc, qt=qt, m=m: e.reciprocal(out=zc[:, m * 4 + qt:m * 4 + qt + 1], in_=acc[:, 128:129]), reads=[Tps[4 + qt]], writes=[Tzc])
                        if m == 0:
                            pass
                    if m == 0:
                        for qt in range(4):
                            acc = psb[4 + qt]
                            k.op("dve", lambda e, acc=acc, qt=qt: e.tensor_scalar(out=QP["t1"][0][:, qt * 128:(qt + 1) * 128], in0=acc[:, 0:128], scalar1=zc[:, qt:qt + 1], scalar2=None, op0=ALU.mult),
                                 reads=[Tps[4 + qt], Tzc], writes=[QP["T"]["t1"][0]])
                    else:
                        for qt in range(4):
                            acc = psb[4 + qt]
                            i2 = qt % 2
                            k.op("dve", lambda e, acc=acc, qt=qt, i2=i2: e.tensor_scalar(out=o01[i2], in0=acc[:, 0:128], scalar1=zc[:, 4 + qt:5 + qt], scalar2=lamt[:, 4:5], op0=ALU.mult, op1=ALU.mult),
                                 reads=[Tps[4 + qt], Tzc, Tsm], writes=[To01[i2]])
                            k.op("dve", lambda e, qt=qt, i2=i2: e.tensor_tensor(out=o01[i2], in0=o01[i2], in1=QP["t1"][0][:, qt * 128:(qt + 1) * 128], op=ALU.add),
                                 reads=[To01[i2], QP["T"]["t1"][0]], writes=[To01[i2]])
                            k.op("act", lambda e, qt=qt, i2=i2: e.activation(out=junk, in_=o01[i2], func=AF.Square, accum_out=QP["rs"][0][:, qt:qt + 1]),
                                 reads=[To01[i2]], writes=[Tjunk, QP["T"]["rs"][0]])
                            k.op("act", lambda e, qt=qt: e.activation(out=QP["rs"][0][:, qt:qt + 1], in_=QP["rs"][0][:, qt:qt + 1], func=AF.Sqrt, scale=1.0 / 128, bias=epsc),
                                 reads=[QP["T"]["rs"][0], Tconst], writes=[QP["T"]["rs"][0]])
                            k.op("dve", lambda e, qt=qt: e.reciprocal(out=QP["rs"][0][:, qt:qt + 1], in_=QP["rs"][0][:, qt:qt + 1]), reads=[QP["T"]["rs"][0]], writes=[QP["T"]["rs"][0]])
                            k.op("dve", lambda e, qt=qt, i2=i2: e.tensor_scalar(out=o01[i2], in0=o01[i2], scalar1=QP["rs"][0][:, qt:qt + 1], scalar2=(1.0 - lam_init), op0=ALU.mult, op1=ALU.mult),
                                 reads=[To01[i2], QP["T"]["rs"][0]], writes=[To01[i2]])
                            k.op("dve", lambda e, qt=qt, i2=i2, oc=oc: e.tensor_tensor(out=oc[:, qt * 128:(qt + 1) * 128], in0=o01[i2], in1=subln, op=ALU.mult),
                                 reads=[To01[i2], Tsm], writes=[Toch[qd % 2]])
                b = bank()
                pT = psb[b][:].bitcast(BF16)
                for qt in range(4):
                    k.op("pe", lambda e, qt=qt, pT=pT, oc=oc: e.transpose(out=pT[:, qt * 128:(qt + 1) * 128], in_=oc[:, qt * 128:(qt + 1) * 128], identity=ident),
                         reads=[Toch[qd % 2], Tconst], writes=[Tps[b]])
                si = qd % 2
                k.op("act", lambda e, pT=pT, si=si: e.activation(out=ocTs[si], in_=pT[:, 0:512], func=AF.Copy), reads=[Tps[b]], writes=[TocTs[si]])
                k.dma("sp", oT[2][h][:, qd * 512:(qd + 1) * 512], ocTs[si], f"ocTs{si}", reads=[TocTs[si]], writes=[ToT["c"][qd]])
        set_banks(range(8))
        k.barrier()

        if stop <= 4:
            return
        HS = bfv(HT_OFF, 8192).rearrange("p (a c) -> p a c", a=16)
        HD = bfv(HT_OFF + 4096, 8192).rearrange("p (a c) -> p a c", a=16)
        THS, THD = T(), T()
        ro = R0
        zT = f32v(ro, 2048); ro += 2048
        h1 = f32v(ro, 2048); ro += 2048
        h2 = f32v(ro, 2048); ro += 2048
        h2b = bfv(ro, 2048); ro += 1024
        w3b = bfv(ro, 2048); ro += 1024
        st3 = f32v(ro, 512); ro += 512
        dec = [f32v(ro + 512 * i, 512) for i in range(2)]; ro += 1024
        fd = f32v(ro, 512); ro += 512
        bd = f32v(ro, 512); ro += 512
        hsf = f32v(ro, 512); ro += 512
        hdf = f32v(ro, 512); ro += 512
        sqs = bfv(ro, 512); ro += 256
        sqd = bfv(ro, 512); ro += 256
        rnb = f32v(ro, 512); ro += 512
        tcb = [bfv(ro + 1088 * i, 2176).rearrange("p (a m) -> p a m", a=17) for i in range(2)]; ro += 2176
        tsb = [bfv(ro + 1088 * i, 2176).rearrange("p (a m) -> p a m", a=17) for i in range(2)]; ro += 2176
        cfs = [f32v(ro + 512 * i, 512) for i in range(4)]; ro += 2048
        assert ro <= AW, ro
        TzT, Th1, Th2, Th2b, Tw3, Tst3 = T(), T(), T(), T(), T(), T()
        Tdec = [T(), T()]
        Tfd, Tbd, Thsf, Thdf, Tsqs, Tsqd, Trnb = T(), T(), T(), T(), T(), T(), T()
        Ttc, Tts = [T(), T()], [T(), T()]
        Tcfs = [T() for _ in range(4)]
        Tcf = [[[T() for _ in range(2)] for _ in range(17)] for _ in range(2)]
        for q in range(4):
            k.dma("sp", zT[0:33, q * 512:(q + 1) * 512], C["c_zT"][:, q * 512:(q + 1) * 512], "zT", writes=[TzT])
            k.dma("pool", w3b[0:64, q * 512:(q + 1) * 512], P["filt_w3"][l][:, q * 512:(q + 1) * 512], "w3", writes=[Tw3])

        for q in range(4):
            b = bank()
            k.op("pe", lambda e, b=b, q=q: e.matmul(psb[b][0:64, :], lhsT=fw1[0:33, :], rhs=zT[0:33, q * 512:(q + 1) * 512], start=True, stop=True),
                 reads=[Tsm, TzT], writes=[Tps[b]])
            k.op("act", lambda e, b=b, q=q: e.activation(out=h1[0:64, q * 512:(q + 1) * 512], in_=psb[b][0:64, :], func=AF.Sin, scale=fvec[0:64, 4:5], bias=fvec[0:64, 5:6]),
                 reads=[Tps[b], Tsm], writes=[Th1])
        for q in range(4):
            sl_ = slice(q * 512, (q + 1) * 512)
            k.op("dve", lambda e, sl_=sl_: e.tensor_tensor(out=h2[0:64, sl_], in0=h1[0:64, sl_], in1=h1[0:64, sl_], op=ALU.mult), reads=[Th1], writes=[Th2])
            k.op("dve", lambda e, sl_=sl_: e.tensor_scalar(out=h2[0:64, sl_], in0=h2[0:64, sl_], scalar1=-4.0, scalar2=3.0, op0=ALU.mult, op1=ALU.add), reads=[Th2], writes=[Th2])
        for q in range(4):
            sl_ = slice(q * 512, (q + 1) * 512)
            k.op("dve", lambda e, sl_=sl_: e.tensor_tensor(out=h1[0:64, sl_], in0=h1[0:64, sl_], in1=h2[0:64, sl_], op=ALU.mult), reads=[Th1, Th2], writes=[Th1])
        for q in range(4):
            b = bank()
            k.op("pe", lambda e, b=b, q=q: e.matmul(psb[b][0:64, :], lhsT=fw2[0:64, :], rhs=h1[0:64, q * 512:(q + 1) * 512], start=True, stop=True),
                 reads=[Tsm, Th1], writes=[Tps[b]])
            k.op("act", lambda e, b=b, q=q: e.activation(out=h2[0:64, q * 512:(q + 1) * 512], in_=psb[b][0:64, :], func=AF.Sin, scale=fvec[0:64, 6:7], bias=fvec[0:64, 7:8]),
                 reads=[Tps[b], Tsm], writes=[Th2])
        for q in range(4):
            sl_ = slice(q * 512, (q + 1) * 512)
            k.op("dve", lambda e, sl_=sl_: e.tensor_tensor(out=h1[0:64, sl_], in0=h2[0:64, sl_], in1=h2[0:64, sl_], op=ALU.mult), reads=[Th2], writes=[Th1])
            k.op("dve", lambda e, sl_=sl_: e.tensor_scalar(out=h1[0:64, sl_], in0=h1[0:64, sl_], scalar1=-4.0, scalar2=3.0, op0=ALU.mult, op1=ALU.add), reads=[Th1], writes=[Th1])
        for q in range(4):
            sl_ = slice(q * 512, (q + 1) * 512)
            k.op("dve", lambda e, sl_=sl_: e.tensor_tensor(out=h2b[0:64, sl_], in0=h2[0:64, sl_], in1=h1[0:64, sl_], op=ALU.mult), reads=[Th1, Th2], writes=[Th2b])
        decv = C["c_decay"]
        set_banks([0, 1, 2, 3, 4, 5, 6])
        for od in range(2):
            for a in range(16):
                di = a % 2
                k.dma("sp", dec[di], decv[:, a, :], f"dec{di}", writes=[Tdec[di]])
                bf_ = bank()
                k.op("pe", lambda e, b=bf_, a=a, od=od: e.matmul(psb[b][:], lhsT=h2b[0:64, a * 128:(a + 1) * 128], rhs=w3b[0:64, (2 * od) * 512:(2 * od + 1) * 512], start=True, stop=True),
                     reads=[Th2b, Tw3], writes=[Tps[bf_]])
                bb_ = bank()
                k.op("pe", lambda e, b=bb_, a=a, od=od: e.matmul(psb[b][:], lhsT=h2b[0:64, a * 128:(a + 1) * 128], rhs=w3b[0:64, (2 * od + 1) * 512:(2 * od + 2) * 512], start=True, stop=True),
                     reads=[Th2b, Tw3], writes=[Tps[bb_]])
                k.op("dve", lambda e, b=bf_, di=di: e.tensor_tensor(out=fd, in0=psb[b][:], in1=dec[di], op=ALU.mult), reads=[Tps[bf_], Tdec[di]], writes=[Tfd])
                k.op("dve", lambda e, b=bb_, di=di: e.tensor_tensor(out=bd, in0=psb[b][:], in1=dec[di], op=ALU.mult), reads=[Tps[bb_], Tdec[di]], writes=[Tbd])
                k.op("dve", lambda e: e.tensor_tensor(out=hsf, in0=fd, in1=bd, op=ALU.add), reads=[Tfd, Tbd], writes=[Thsf])
                k.op("dve", lambda e: e.tensor_tensor(out=hdf, in0=fd, in1=bd, op=ALU.subtract), reads=[Tfd, Tbd], writes=[Thdf])
                k.op("act", lambda e, a=a: e.activation(out=HS[:, a, :], in_=hsf, func=AF.Copy), reads=[Thsf], writes=[THS])
                k.op("act", lambda e, a=a: e.activation(out=HD[:, a, :], in_=hdf, func=AF.Copy), reads=[Thdf], writes=[THD])
                k.op("act", lambda e: e.activation(out=sqs, in_=hsf, func=AF.Square), reads=[Thsf], writes=[Tsqs])
                k.op("act", lambda e: e.activation(out=sqd, in_=hdf, func=AF.Square), reads=[Thdf], writes=[Tsqd])
                k.op("pe", lambda e, a=a: e.matmul(psb[7][:], lhsT=nrm[:, 0 if a == 0 else 2, :], rhs=sqs, start=(a == 0), stop=False), reads=[Tsqs, Tconst], writes=[Tps[7]])
                k.op("pe", lambda e, a=a: e.matmul(psb[7][:], lhsT=nrm[:, 1 if a == 0 else 2, :], rhs=sqd, start=False, stop=(a == 15)), reads=[Tsqd, Tconst], writes=[Tps[7]])
            k.op("act", lambda e: e.activation(out=rnb, in_=psb[7][:], func=AF.Sqrt, bias=epsc), reads=[Tps[7], Tconst], writes=[Trnb])
            k.op("dve", lambda e: e.reciprocal(out=rnb, in_=rnb), reads=[Trnb], writes=[Trnb])
            for b_ in range(17):
                ti = b_ % 2
                k.dma("sp", tcb[ti].rearrange("p a m -> p (a m)")[:, 0:1088], C["c_tc"][b_][:, 0:1088], f"tc{ti}", writes=[Ttc[ti]])
                k.dma("sp", tcb[ti].rearrange("p a m -> p (a m)")[:, 1088:2176], C["c_tc"][b_][:, 1088:2176], f"tc{ti}", writes=[Ttc[ti]])
                if b_ < 16:
                    k.dma("sp", tsb[ti].rearrange("p a m -> p (a m)")[:, 0:1088], C["c_ts"][b_][:, 0:1088], f"ts{ti}", writes=[Tts[ti]])
                    k.dma("sp", tsb[ti].rearrange("p a m -> p (a m)")[:, 1088:2176], C["c_ts"][b_][:, 1088:2176], f"ts{ti}", writes=[Tts[ti]])
                np_ = 128 if b_ < 16 else 1
                pb = bank()
                for a in range(16):
                    k.op("pe", lambda e, pb=pb, a=a, ti=ti, np_=np_: e.matmul(psb[pb][0:np_, :], lhsT=tcb[ti][:, a, 0:np_], rhs=HS[:, a, :], start=(a == 0), stop=(a == 15)),
                         reads=[Ttc[ti], THS], writes=[Tps[pb]])
                ci = (2 * b_) % 4
                k.op("dve", lambda e, pb=pb, ci=ci, np_=np_, b_=b_: e.scalar_tensor_tensor(out=cfs[ci][0:np_, :], in0=psb[pb][0:np_, :], scalar=wf[0:np_, b_:b_ + 1], in1=rnb[0:np_, :], op0=ALU.mult, op1=ALU.mult),
                     reads=[Tps[pb], Tconst, Trnb], writes=[Tcfs[ci]])
                k.dma("sp", cf[od, b_, 0, 0:np_, :], cfs[ci][0:np_, :], f"cfs{ci}", reads=[Tcfs[ci]], writes=[Tcf[od][b_][0]])
                if b_ < 16:
                    pb2 = bank()
                    for a in range(16):
                        k.op("pe", lambda e, pb2=pb2, a=a, ti=ti: e.matmul(psb[pb2][:], lhsT=tsb[ti][:, a, 0:128], rhs=HD[:, a, :], start=(a == 0), stop=(a == 15)),
                             reads=[Tts[ti], THD], writes=[Tps[pb2]])
                    ci2 = ci + 1
                    k.op("dve", lambda e, pb2=pb2, ci2=ci2, b_=b_: e.scalar_tensor_tensor(out=cfs[ci2], in0=psb[pb2][:], scalar=wf[:, b_:b_ + 1], in1=rnb, op0=ALU.mult, op1=ALU.mult),
                         reads=[Tps[pb2], Tconst, Trnb], writes=[Tcfs[ci2]])
                    k.dma("sp", cf[od, b_, 1], cfs[ci2], f"cfs{ci2}", reads=[Tcfs[ci2]], writes=[Tcf[od][b_][1]])
        set_banks(range(8))
        k.barrier()

        if stop <= 5:
            return
        VT = bfv(HT_OFF, 8192).rearrange("p (a c) -> p a c", a=16)
        ZT = bfv(HT_OFF + 4096, 8192).rearrange("p (a c) -> p a c", a=16)
        TVT, TZT = T(), T()
        ro = R0
        PY = bfv(ro, 17 * 512).rearrange("p (a c) -> p a c", a=17); ro += 17 * 256
        QY = bfv(ro, 16 * 512).rearrange("p (a c) -> p a c", a=16); ro += 16 * 256
        tcb = [bfv(ro + 1088 * i, 2176).rearrange("p (a m) -> p a m", a=17) for i in range(2)]; ro += 2176
        tsb = [bfv(ro + 1088 * i, 2176).rearrange("p (a m) -> p a m", a=17) for i in range(2)]; ro += 2176
        cre = [f32v(ro + 512 * i, 512) for i in range(2)]; ro += 1024
        csn = [f32v(ro + 512 * i, 512) for i in range(2)]; ro += 1024
        vr = f32v(ro, 512); ro += 512
        vs = f32v(ro, 512); ro += 512
        ta = f32v(ro, 512); ro += 512
        tb_ = f32v(ro, 512); ro += 512
        tcc = f32v(ro, 512); ro += 512
        td = f32v(ro, 512); ro += 512
        xg = [bfv(ro + 256 * i, 512) for i in range(2)]; ro += 512
        zo = [bfv(ro + 256 * i, 512) for i in range(2)]; ro += 512
        obTs = [bfv(ro + 1024 * i, 2048).rearrange("p (c t) -> p c t", c=4) for i in range(2)]; ro += 2048
        assert ro <= AW, ro
        TPY, TQY = [T() for _ in range(17)], [T() for _ in range(16)]
        Ttc, Tts = [T(), T()], [T(), T()]
        Tcre, Tcsn = [T(), T()], [T(), T()]
        Tvr, Tvs, Tta, Ttb, Ttcc, Ttd = T(), T(), T(), T(), T(), T()
        Txg, Tzo, TobTs = [T(), T()], [T(), T()], [T(), T()]
        for a4 in range(4):
            k.dma("sp", VT[:, a4 * 4:(a4 + 1) * 4, :], utok[0].rearrange("(a p) c -> p a c", p=128)[:, a4 * 4:(a4 + 1) * 4, :], "vt", reads=Tutok[0:4], writes=[TVT])
        for od in range(2):
            src = VT if od == 0 else ZT
            Tsrc = TVT if od == 0 else TZT
            for b_ in range(17):
                ti = b_ % 2
                np_ = 128 if b_ < 16 else 1
                k.dma("sp", tcb[ti].rearrange("p a m -> p (a m)")[:, 0:1088], C["c_tc"][b_][:, 0:1088], f"tc{ti}", writes=[Ttc[ti]])
                k.dma("sp", tcb[ti].rearrange("p a m -> p (a m)")[:, 1088:2176], C["c_tc"][b_][:, 1088:2176], f"tc{ti}", writes=[Ttc[ti]])
                k.dma("sp", cre[ti][0:np_, :], cf[od, b_, 0, 0:np_, :], f"cre{ti}", reads=[Tcf[od][b_][0]], writes=[Tcre[ti]])
                if b_ < 16:
                    k.dma("sp", tsb[ti].rearrange("p a m -> p (a m)")[:, 0:1088], C["c_ts"][b_][:, 0:1088], f"ts{ti}", writes=[Tts[ti]])
                    k.dma("sp", tsb[ti].rearrange("p a m -> p (a m)")[:, 1088:2176], C["c_ts"][b_][:, 1088:2176], f"ts{ti}", writes=[Tts[ti]])
                    k.dma("sp", csn[ti], cf[od, b_, 1], f"csn{ti}", reads=[Tcf[od][b_][1]], writes=[Tcsn[ti]])
                pr = bank()
                for a in range(16):
                    k.op("pe", lambda e, pr=pr, a=a, ti=ti, np_=np_, src=src: e.matmul(psb[pr][0:np_, :], lhsT=tcb[ti][:, a, 0:np_], rhs=src[:, a, :], start=(a == 0), stop=(a == 15)),
                         reads=[Ttc[ti], Tsrc], writes=[Tps[pr]])
                if b_ < 16:
                    pi = bank()
                    for a in range(16):
                        k.op("pe", lambda e, pi=pi, a=a, ti=ti, src=src: e.matmul(psb[pi][:], lhsT=tsb[ti][:, a, 0:128], rhs=src[:, a, :], start=(a == 0), stop=(a == 15)),
                             reads=[Tts[ti], Tsrc], writes=[Tps[pi]])
                    k.op("act", lambda e, pr=pr: e.activation(out=vr, in_=psb[pr][:], func=AF.Copy), reads=[Tps[pr]], writes=[Tvr])
                    k.op("act", lambda e, pi=pi: e.activation(out=vs, in_=psb[pi][:], func=AF.Copy), reads=[Tps[pi]], writes=[Tvs])
                    k.op("dve", lambda e, ti=ti: e.tensor_tensor(out=ta, in0=vr, in1=cre[ti], op=ALU.mult), reads=[Tvr, Tcre[ti]], writes=[Tta])
                    k.op("dve", lambda e, ti=ti: e.tensor_tensor(out=tb_, in0=vs, in1=csn[ti], op=ALU.mult), reads=[Tvs, Tcsn[ti]], writes=[Ttb])
                    k.op("dve", lambda e, b_=b_: e.tensor_tensor(out=PY[:, b_, :], in0=ta, in1=tb_, op=ALU.subtract), reads=[Tta, Ttb], writes=[TPY[b_]])
                    k.op("dve", lambda e, ti=ti: e.tensor_tensor(out=tcc, in0=vr, in1=csn[ti], op=ALU.mult), reads=[Tvr, Tcsn[ti]], writes=[Ttcc])
                    k.op("dve", lambda e, ti=ti: e.tensor_tensor(out=td, in0=vs, in1=cre[ti], op=ALU.mult), reads=[Tvs, Tcre[ti]], writes=[Ttd])
                    k.op("dve", lambda e, b_=b_: e.tensor_tensor(out=QY[:, b_, :], in0=tcc, in1=td, op=ALU.add), reads=[Ttcc, Ttd], writes=[TQY[b_]])
                else:
                    k.op("dve", lambda e, pr=pr, ti=ti: e.tensor_tensor(out=PY[0:1, 16, :], in0=psb[pr][0:1, :], in1=cre[ti][0:1, :], op=ALU.mult), reads=[Tps[pr], Tcre[ti]], writes=[TPY[16]])
            for B in range(16):
                ti = B % 2
                k.dma("sp", tcb[ti].rearrange("p a m -> p (a m)")[:, 0:1088], C["c_tc"][B][:, 0:1088], f"tc{ti}", writes=[Ttc[ti]])
                k.dma("sp", tcb[ti].rearrange("p a m -> p (a m)")[:, 1088:2176], C["c_tc"][B][:, 1088:2176], f"tc{ti}", writes=[Ttc[ti]])
                k.dma("sp", tsb[ti].rearrange("p a m -> p (a m)")[:, 0:1088], C["c_ts"][B][:, 0:1088], f"ts{ti}", writes=[Tts[ti]])
                k.dma("sp", tsb[ti].rearrange("p a m -> p (a m)")[:, 1088:2176], C["c_ts"][B][:, 1088:2176], f"ts{ti}", writes=[Tts[ti]])
                xi = B % 2
                k.dma("sp", xg[xi], utok[1 + od][B * 128:(B + 1) * 128, :], f"xg{xi}", reads=Tutok[4 * (1 + od):4 * (2 + od)], writes=[Txg[xi]])
                pb = bank()
                for a in range(17):
                    np_ = 128 if a < 16 else 1
                    k.op("pe", lambda e, pb=pb, a=a, ti=ti, np_=np_: e.matmul(psb[pb][:], lhsT=tcb[ti][0:np_, a, :], rhs=PY[0:np_, a, :], start=(a == 0), stop=False),
                         reads=[Ttc[ti], TPY[a]], writes=[Tps[pb]])
                for a in range(16):
                    k.op("pe", lambda e, pb=pb, a=a, ti=ti: e.matmul(psb[pb][:], lhsT=tsb[ti][:, a, :], rhs=QY[:, a, :], start=False, stop=(a == 15)),
                         reads=[Tts[ti], TQY[a]], writes=[Tps[pb]])
                k.op("dve", lambda e, B=B, od=od, src=src: e.tensor_tensor(out=ta, in0=src[:, B, :], in1=hyd[:, od, :], op=ALU.mult), reads=[Tsrc, Tsm], writes=[Tta])
                k.op("dve", lambda e, pb=pb: e.tensor_tensor(out=ta, in0=psb[pb][:], in1=ta, op=ALU.add), reads=[Tps[pb], Tta], writes=[Tta])
                if od == 0:
                    k.op("dve", lambda e, B=B, xi=xi: e.tensor_tensor(out=ZT[:, B, :], in0=ta, in1=xg[xi], op=ALU.mult), reads=[Tta, Txg[xi]], writes=[TZT])
                else:
                    zi = B % 2
                    k.op("dve", lambda e, xi=xi, zi=zi: e.tensor_tensor(out=zo[zi], in0=ta, in1=xg[xi], op=ALU.mult), reads=[Tta, Txg[xi]], writes=[Tzo[zi]])
                    b = bank()
                    pT = psb[b][:].bitcast(BF16)
                    for c4 in range(4):
                        k.op("pe", lambda e, c4=c4, pT=pT, zi=zi: e.transpose(out=pT[:, c4 * 128:(c4 + 1) * 128], in_=zo[zi][:, c4 * 128:(c4 + 1) * 128], identity=ident),
                             reads=[Tzo[zi], Tconst], writes=[Tps[b]])
                    si = (B // 4) % 2
                    k.op("act", lambda e, pT=pT, si=si, B=B: e.activation(out=obTs[si][:, :, (B % 4) * 128:(B % 4 + 1) * 128], in_=pT[:, 0:512].rearrange("p (c t) -> p c t", c=4), func=AF.Copy),
                         reads=[Tps[b]], writes=[TobTs[si]])
                    if B % 4 == 3:
                        q = B // 4
                        k.dma("sp", oT[1].rearrange("c p t -> p c t")[:, :, q * 512:(q + 1) * 512], obTs[si], f"obTs{si}", reads=[TobTs[si]], writes=[ToT["b"][q]])
        k.barrier()

        if stop <= 6:
            return
        WO = [bfv(HT_OFF + 2048 * i, 4096).rearrange("p (c n) -> p c n", c=4) for i in range(3)]
        TWO = [T(), T(), T()]
        ro = R0
        WOUT = bfv(ro, 8192).rearrange("p (c n) -> p c n", c=8); ro += 4096
        TWOUT = T()
        oTt = [[bfv(ro + (3 * i + j) * 1024, 2048).rearrange("p (c t) -> p c t", c=4) for j in range(3)] for i in range(2)]; ro += 6144
        ToTt = [[T() for _ in range(3)] for _ in range(2)]
        gt = [bfv(ro + 2048 * j, 4096).rearrange("p (c t) -> p c t", c=8) for j in range(3)]; ro += 6144
        Tgt = [T(), T(), T()]
        mT = bfv(ro, 4096).rearrange("p (c t) -> p c t", c=8); ro += 2048
        TmT = T()
        m1 = f32v(ro, 512); ro += 512
        m2 = f32v(ro, 512); ro += 512
        Tm1, Tm2 = T(), T()
        assert ro <= AW, ro
        for i, nm in enumerate(("w_oa", "w_ob", "w_oc")):
            for c4 in range(4):
                k.dma("pool", WO[i][:, c4, :], P[nm][l][c4 * 128:(c4 + 1) * 128, :], f"wo{i}", writes=[TWO[i]])
        for c8 in range(8):
            k.dma("pool", WOUT[:, c8, :], P["w_out"][l][c8 * 128:(c8 + 1) * 128, :], "wout", writes=[TWOUT])
        for q in range(4):
            qi = q % 2
            for i, nm in enumerate("abc"):
                k.dma("sp", oTt[qi][i], oT[i].rearrange("c p t -> p c t")[:, :, q * 512:(q + 1) * 512], f"oTt{qi}{i}", reads=[ToT[nm][q]], writes=[ToTt[qi][i]])
                for g2 in range(2):
                    k.dma("sp", gt[i][:, 4 * g2:4 * g2 + 4, :], gatesT[8 * i + 4 * g2:8 * i + 4 * g2 + 4].rearrange("c p t -> p c t")[:, :, q * 512:(q + 1) * 512], f"gt{i}", reads=Tgates[8 * i:8 * (i + 1)], writes=[Tgt[i]])
            for dj in range(8):
                bs = []
                for i in range(3):
                    b = bank()
                    bs.append(b)
                    for kc in range(4):
                        k.op("pe", lambda e, b=b, i=i, kc=kc, dj=dj, qi=qi: e.matmul(psb[b][:], lhsT=WO[i][:, kc, dj * 128:(dj + 1) * 128], rhs=oTt[qi][i][:, kc, :], start=(kc == 0), stop=(kc == 3)),
                             reads=[TWO[i], ToTt[qi][i]], writes=[Tps[b]])
                k.op("dve", lambda e, b=bs[0], dj=dj: e.tensor_tensor(out=m1, in0=psb[b][:], in1=gt[0][:, dj, :], op=ALU.mult), reads=[Tps[bs[0]], Tgt[0]], writes=[Tm1])
                k.op("dve", lambda e, b=bs[1], dj=dj: e.tensor_tensor(out=m2, in0=psb[b][:], in1=gt[1][:, dj, :], op=ALU.mult), reads=[Tps[bs[1]], Tgt[1]], writes=[Tm2])
                k.op("dve", lambda e: e.tensor_tensor(out=m1, in0=m1, in1=m2, op=ALU.add), reads=[Tm1, Tm2], writes=[Tm1])
                k.op("dve", lambda e, b=bs[2], dj=dj: e.tensor_tensor(out=m2, in0=psb[b][:], in1=gt[2][:, dj, :], op=ALU.mult), reads=[Tps[bs[2]], Tgt[2]], writes=[Tm2])
                k.op("dve", lambda e, dj=dj: e.tensor_tensor(out=mT[:, dj, :], in0=m1, in1=m2, op=ALU.add), reads=[Tm1, Tm2], writes=[TmT])
            for tt in range(4):
                n = q * 4 + tt
                for hf in range(2):
                    b = bank()
                    for kc in range(8):
                        k.op("pe", lambda e, b=b, kc=kc, tt=tt, hf=hf: e.matmul(psb[b][:], lhsT=mT[:, kc, tt * 128:(tt + 1) * 128], rhs=WOUT[:, kc, hf * 512:(hf + 1) * 512], start=(kc == 0), stop=(kc == 7)),
                             reads=[TmT, TWOUT], writes=[Tps[b]])
                    k.op("dve", lambda e, b=b, n=n, hf=hf: e.tensor_tensor(out=XRES[:, n, hf * 512:(hf + 1) * 512], in0=psb[b][:], in1=XRES[:, n, hf * 512:(hf + 1) * 512], op=ALU.add),
                         reads=[Tps[b], Tx[n]], writes=[Tx[n]])
        k.barrier()

        if stop <= 7:
            return
        rms_to_hT("norm2_g")
        k.barrier()

        if stop <= 8:
            return
        ro = R0
        W2 = bfv(ro, NFC * 1024).rearrange("p (f n) -> p f n", f=NFC); ro += NFC * 512
        TW2 = T()
        gT = bfv(ro, NFC * 512).rearrange("p (f t) -> p f t", f=NFC); ro += NFC * 256
        TgT = T()
        w13 = [[bfv(ro + (2 * i + j) * 512, 1024).rearrange("p (c n) -> p c n", c=8) for j in range(2)] for i in range(2)]; ro += 2048
        Tw13 = [[T(), T()], [T(), T()]]
        sl = [f32v(ro + 512 * i, 512) for i in range(2)]; ro += 1024
        Tsl = [T(), T()]
        assert ro <= AW, ro
        for fc in range(NFC):
            k.dma("pool", W2[:, fc, :], P["ffn_w2"][l][fc * 128:(fc + 1) * 128, :], "w2", writes=[TW2])
        for q in range(4):
            for fc in range(NFC):
                wi = fc % 2
                k.dma("pool", w13[wi][0], P["ffn_w1"][l][:, fc * 128:(fc + 1) * 128].rearrange("(c p) n -> p c n", p=128), f"w1_{wi}", writes=[Tw13[wi][0]])
                k.dma("pool", w13[wi][1], P["ffn_w3"][l][:, fc * 128:(fc + 1) * 128].rearrange("(c p) n -> p c n", p=128), f"w3_{wi}", writes=[Tw13[wi][1]])
                b1 = bank()
                for kc in range(8):
                    k.op("pe", lambda e, b=b1, kc=kc, wi=wi, q=q: e.matmul(psb[b][:], lhsT=w13[wi][0][:, kc, :], rhs=HT[:, kc, q * 512:(q + 1) * 512], start=(kc == 0), stop=(kc == 7)),
                         reads=[Tw13[wi][0], ThT], writes=[Tps[b1]])
                b3 = bank()
                for kc in range(8):
                    k.op("pe", lambda e, b=b3, kc=kc, wi=wi, q=q: e.matmul(psb[b][:], lhsT=w13[wi][1][:, kc, :], rhs=HT[:, kc, q * 512:(q + 1) * 512], start=(kc == 0), stop=(kc == 7)),
                         reads=[Tw13[wi][1], ThT], writes=[Tps[b3]])
                si = fc % 2
                k.op("act", lambda e, b=b1, si=si: e.activation(out=sl[si], in_=psb[b][:], func=AF.Silu), reads=[Tps[b1]], writes=[Tsl[si]])
                k.op("dve", lambda e, b=b3, si=si, fc=fc: e.tensor_tensor(out=gT[:, fc, :], in0=psb[b][:], in1=sl[si], op=ALU.mult), reads=[Tps[b3], Tsl[si]], writes=[TgT])
            for tt in range(4):
                n = q * 4 + tt
                for hf in range(2):
                    b = bank()
                    for fc in range(NFC):
                        k.op("pe", lambda e, b=b, fc=fc, tt=tt, hf=hf: e.matmul(psb[b][:], lhsT=gT[:, fc, tt * 128:(tt + 1) * 128], rhs=W2[:, fc, hf * 512:(hf + 1) * 512], start=(fc == 0), stop=(fc == NFC - 1)),
                             reads=[TgT, TW2], writes=[Tps[b]])
                    k.op("dve", lambda e, b=b, n=n, hf=hf: e.tensor_tensor(out=XRES[:, n, hf * 512:(hf + 1) * 512], in0=psb[b][:], in1=XRES[:, n, hf * 512:(hf + 1) * 512], op=ALU.add),
                         reads=[Tps[b], Tx[n]], writes=[Tx[n]])
        k.barrier()

    for l in layers:
        layer(l)

    yv = y_out.rearrange("(n p) d -> p n d", p=128)
    outs = []
    for n in range(NT):
        outs.append(k.dma("sp", yv[:, n, :], XRES[:, n, :], f"y{n % 4}", reads=[Tx[n]]))
    k.emit(final_waits=outs)
    es.close()
    return nc


_CACHE = {}


def _run(inputs, layers, dbg=False):
    consts = _CACHE.setdefault("consts", make_consts())
    key = (tuple(layers), dbg)
    if key not in _CACHE:
        _CACHE[key] = build(layers, dbg)
    nc = _CACHE[key]
    x = np.ascontiguousarray(np.asarray(inputs["x"], dtype=np.float32))
    in_maps = []
    for b in range(8):
        m = {"x": x[b]}
        for n in PARAM_SHAPES:
            m[n] = np.ascontiguousarray(np.asarray(inputs[n], dtype=np.float32))
        m.update(consts)
        in_maps.append(m)
    res = run_bass_kernel_spmd(nc, in_maps, core_ids=list(range(8)))
    return res


def kernel(**inputs):
    res = _run(inputs, (0, 1))
    return np.stack([np.asarray(r["y"], dtype=np.float32) for r in res.results], axis=0)
```
